# Optimizing a Trainium2 kernel written in Bass

```python
import math
import jax, jax.numpy as jnp
from jax import lax
import numpy as np

D_MODEL = 2048
BATCH = 2
SEQ = 16384
DEPTH = 2

CHUNK = 64
N_A_LAYERS = DEPTH // 2
N_B_LAYERS = DEPTH - N_A_LAYERS
SSM_GROUP = 16
SSM_GROUPS = D_MODEL // SSM_GROUP
SSM_STATE = 64
SCAN_BLOCK = 128
DT_MIN = 1e-3
DT_MAX = 1e-1
N_HEADS = 16
Q_LORA_RANK = 512
KV_LORA_RANK = 512
QK_NOPE = 128
QK_ROPE = 64
V_HEAD = 128
ROPE_THETA = 10000.0
ATTN_SCALE = (QK_NOPE + QK_ROPE) ** -0.5
QBLK = 128
D_FF = 5632
CONV_W = 3
EPS = 1e-6

kernel_name = "s5_mla_yoco_sandwich_convffn"


def rmsnorm(x, g):
    x32 = x.astype(jnp.float32)
    y = x32 * lax.rsqrt(jnp.mean(x32 * x32, axis=-1, keepdims=True) + EPS) * g.astype(jnp.float32)
    return y.astype(x.dtype)


def rope_cos_sin(positions):
    freqs = ROPE_THETA ** (-jnp.arange(0, QK_ROPE, 2, dtype=jnp.float32) / QK_ROPE)
    ang = positions.astype(jnp.float32)[..., None] * freqs
    return jnp.cos(ang), jnp.sin(ang)


def apply_rope(x, cos, sin):
    x32 = x.astype(jnp.float32)
    x1, x2 = jnp.split(x32, 2, axis=-1)
    out = jnp.concatenate([x1 * cos - x2 * sin, x2 * cos + x1 * sin], axis=-1)
    return out.astype(x.dtype)


def s5_mixer(u, a_re, a_im, log_dt, b_re, b_im, c_re, c_im, d_skip, w_glu, b_glu):
    dtype = u.dtype
    Bsz, S, _ = u.shape
    f32 = jnp.float32
    a_re, a_im = a_re.astype(f32), a_im.astype(f32)
    b_re, b_im = b_re.astype(f32), b_im.astype(f32)
    c_re, c_im = c_re.astype(f32), c_im.astype(f32)
    d_skip = d_skip.astype(f32)
    dt = jnp.exp(log_dt.astype(f32))[:, None]
    mag = jnp.exp(dt * a_re)
    ab_re = mag * jnp.cos(dt * a_im)
    ab_im = mag * jnp.sin(dt * a_im)
    den = a_re * a_re + a_im * a_im
    nr, ni = ab_re - 1.0, ab_im
    coef_re = ((nr * a_re + ni * a_im) / den)[..., None]
    coef_im = ((ni * a_re - nr * a_im) / den)[..., None]
    bb_re = coef_re * b_re - coef_im * b_im
    bb_im = coef_re * b_im + coef_im * b_re

    nblk = S // SCAN_BLOCK
    u32 = u.astype(f32).reshape(Bsz, nblk, SCAN_BLOCK, SSM_GROUPS, SSM_GROUP).transpose(1, 0, 2, 3, 4)

    def combine(e1, e2):
        ar1, ai1, br1, bi1 = e1
        ar2, ai2, br2, bi2 = e2
        return (ar2 * ar1 - ai2 * ai1, ar2 * ai1 + ai2 * ar1,
                ar2 * br1 - ai2 * bi1 + br2, ar2 * bi1 + ai2 * br1 + bi2)

    def step(carry, ub):
        h_re, h_im = carry
        x_re = jnp.einsum('blgp,gnp->blgn', ub, bb_re)
        x_im = jnp.einsum('blgp,gnp->blgn', ub, bb_im)
        a_r = jnp.broadcast_to(ab_re, x_re.shape)
        a_i = jnp.broadcast_to(ab_im, x_re.shape)
        acc_re, acc_im, hl_re, hl_im = lax.associative_scan(combine, (a_r, a_i, x_re, x_im), axis=1)
        hs_re = acc_re * h_re[:, None] - acc_im * h_im[:, None] + hl_re
        hs_im = acc_re * h_im[:, None] + acc_im * h_re[:, None] + hl_im
        y = (jnp.einsum('blgn,gpn->blgp', hs_re, c_re) - jnp.einsum('blgn,gpn->blgp', hs_im, c_im)
             + d_skip * ub)
        return (hs_re[:, -1], hs_im[:, -1]), y

    h0 = jnp.zeros((Bsz, SSM_GROUPS, SSM_STATE), f32)
    _, ys = lax.scan(step, (h0, h0), u32)
    y = ys.transpose(1, 0, 2, 3, 4).reshape(Bsz, S, D_MODEL)
    y = jax.nn.gelu(y)
    val, gate = jnp.split(y @ w_glu.astype(f32) + b_glu.astype(f32), 2, axis=-1)
    return (val * jax.nn.sigmoid(gate)).astype(dtype)


def shared_kv(x, kv_norm_g, w_dkv, ckv_norm_g, w_ukv, cos, sin):
    Bsz, S, _ = x.shape
    kv_in = rmsnorm(x, kv_norm_g)
    ckr = kv_in @ w_dkv
    c_kv, k_rope = ckr[..., :KV_LORA_RANK], ckr[..., KV_LORA_RANK:]
    c_kv = rmsnorm(c_kv, ckv_norm_g)
    kv = (c_kv @ w_ukv).reshape(Bsz, S, N_HEADS, QK_NOPE + V_HEAD)
    k_nope, v = kv[..., :QK_NOPE], kv[..., QK_NOPE:]
    k_rope = apply_rope(k_rope, cos, sin)
    return k_nope, k_rope, v


def mla_mixer(h, w_dq, cq_norm_g, w_uq, w_o, k_nope, k_rope, v, cos, sin):
    Bsz, S, _ = h.shape
    cq = rmsnorm(h @ w_dq, cq_norm_g)
    q = (cq @ w_uq).reshape(Bsz, S, N_HEADS, QK_NOPE + QK_ROPE)
    q_nope = q[..., :QK_NOPE]
    q_rope = apply_rope(q[..., QK_NOPE:], cos[:, :, None, :], sin[:, :, None, :])
    nq = S // QBLK
    qn = q_nope.reshape(Bsz, nq, QBLK, N_HEADS, QK_NOPE).transpose(1, 0, 2, 3, 4)
    qr = q_rope.reshape(Bsz, nq, QBLK, N_HEADS, QK_ROPE).transpose(1, 0, 2, 3, 4)
    key_chunk = jnp.arange(S) // CHUNK

    def one_block(args):
        qn_b, qr_b, i = args
        s = (jnp.einsum('bqhd,bkhd->bhqk', qn_b, k_nope)
             + jnp.einsum('bqhr,bkr->bhqk', qr_b, k_rope)).astype(jnp.float32) * ATTN_SCALE
        q_chunk = (i * QBLK + jnp.arange(QBLK)) // CHUNK
        mask = key_chunk[None, :] <= q_chunk[:, None]
        p = jax.nn.softmax(jnp.where(mask, s, -1e30), axis=-1).astype(v.dtype)
        return jnp.einsum('bhqk,bkhd->bqhd', p, v)

    out = lax.map(one_block, (qn, qr, jnp.arange(nq)))
    out = out.transpose(1, 0, 2, 3, 4).reshape(Bsz, S, N_HEADS * V_HEAD)
    return out @ w_o


def conv_ffn(h, w_gate, w_up, conv_w, conv_b, w_down):
    g = h @ w_gate
    g = lax.conv_general_dilated(g, conv_w[:, None, :], window_strides=(1,), padding=[(CONV_W - 1, 0)],
                                 dimension_numbers=('NWC', 'WIO', 'NWC'), feature_group_count=D_FF) + conv_b
    return (jax.nn.gelu(g) * (h @ w_up)) @ w_down


def setup_inputs(seed: int = 0) -> dict:
    key = jax.random.key(seed)
    ks = iter(jax.random.split(key, 40))
    f32 = jnp.float32
    nrm = lambda shape, scale: jax.random.normal(next(ks), shape, f32) * scale
    D, G, N, P, H = D_MODEL, SSM_GROUPS, SSM_STATE, SSM_GROUP, N_HEADS
    NA, NB = N_A_LAYERS, N_B_LAYERS
    x = jax.random.normal(next(ks), (BATCH, SEQ, D), f32)
    offset = jax.random.randint(next(ks), (BATCH,), 0, 64) * CHUNK
    positions = (offset[:, None] + jnp.arange(SEQ)[None, :]).astype(jnp.int32)
    norm_g = 1.0 + nrm((DEPTH, 4, D), 0.02)
    n_idx = jnp.arange(N, dtype=f32)
    ssm_a_re = -0.5 * (1.0 + nrm((NA, G, N), 0.02))
    ssm_a_im = math.pi * n_idx + nrm((NA, G, N), 0.01)
    ssm_log_dt = jax.random.uniform(next(ks), (NA, G), f32, math.log(DT_MIN), math.log(DT_MAX))
    ssm_b_re = nrm((NA, G, N, P), (2 * P) ** -0.5)
    ssm_b_im = nrm((NA, G, N, P), (2 * P) ** -0.5)
    ssm_c_re = nrm((NA, G, P, N), (2 * N) ** -0.5)
    ssm_c_im = nrm((NA, G, P, N), (2 * N) ** -0.5)
    ssm_d = nrm((NA, G, P), 1.0)
    ssm_w_glu = nrm((NA, D, 2 * D), D ** -0.5)
    ssm_b_glu = nrm((NA, 2 * D), 0.01)
    kv_norm_g = 1.0 + nrm((D,), 0.02)
    w_dkv = nrm((D, KV_LORA_RANK + QK_ROPE), D ** -0.5)
    ckv_norm_g = 1.0 + nrm((KV_LORA_RANK,), 0.02)
    w_ukv = nrm((KV_LORA_RANK, H * (QK_NOPE + V_HEAD)), KV_LORA_RANK ** -0.5)
    w_dq = nrm((NB, D, Q_LORA_RANK), D ** -0.5)
    cq_norm_g = 1.0 + nrm((NB, Q_LORA_RANK), 0.02)
    w_uq = nrm((NB, Q_LORA_RANK, H * (QK_NOPE + QK_ROPE)), Q_LORA_RANK ** -0.5)
    w_o = nrm((NB, H * V_HEAD, D), (H * V_HEAD) ** -0.5)
    ffn_w_gate = nrm((DEPTH, D, D_FF), D ** -0.5)
    ffn_w_up = nrm((DEPTH, D, D_FF), D ** -0.5)
    ffn_conv_w = nrm((DEPTH, CONV_W, D_FF), CONV_W ** -0.5)
    ffn_conv_b = nrm((DEPTH, D_FF), 0.01)
    ffn_w_down = nrm((DEPTH, D_FF, D), D_FF ** -0.5)
    return {"x": x, "positions": positions, "norm_g": norm_g,
            "ssm_a_re": ssm_a_re, "ssm_a_im": ssm_a_im, "ssm_log_dt": ssm_log_dt,
            "ssm_b_re": ssm_b_re, "ssm_b_im": ssm_b_im, "ssm_c_re": ssm_c_re, "ssm_c_im": ssm_c_im,
            "ssm_d": ssm_d, "ssm_w_glu": ssm_w_glu, "ssm_b_glu": ssm_b_glu,
            "kv_norm_g": kv_norm_g, "w_dkv": w_dkv, "ckv_norm_g": ckv_norm_g, "w_ukv": w_ukv,
            "w_dq": w_dq, "cq_norm_g": cq_norm_g, "w_uq": w_uq, "w_o": w_o,
            "ffn_w_gate": ffn_w_gate, "ffn_w_up": ffn_w_up, "ffn_conv_w": ffn_conv_w,
            "ffn_conv_b": ffn_conv_b, "ffn_w_down": ffn_w_down}


def reference(x, positions, norm_g, ssm_a_re, ssm_a_im, ssm_log_dt, ssm_b_re, ssm_b_im, ssm_c_re, ssm_c_im,
              ssm_d, ssm_w_glu, ssm_b_glu, kv_norm_g, w_dkv, ckv_norm_g, w_ukv, w_dq, cq_norm_g, w_uq, w_o,
              ffn_w_gate, ffn_w_up, ffn_conv_w, ffn_conv_b, ffn_w_down):
    cos, sin = rope_cos_sin(positions)
    for l in range(DEPTH):
        h = rmsnorm(x, norm_g[l, 0])
        if l < N_A_LAYERS:
            m = s5_mixer(h, ssm_a_re[l], ssm_a_im[l], ssm_log_dt[l], ssm_b_re[l], ssm_b_im[l],
                         ssm_c_re[l], ssm_c_im[l], ssm_d[l], ssm_w_glu[l], ssm_b_glu[l])
        else:
            if l == N_A_LAYERS:
                k_nope, k_rope, v = shared_kv(x, kv_norm_g, w_dkv, ckv_norm_g, w_ukv, cos, sin)
            j = l - N_A_LAYERS
            m = mla_mixer(h, w_dq[j], cq_norm_g[j], w_uq[j], w_o[j], k_nope, k_rope, v, cos, sin)
        x = x + rmsnorm(m, norm_g[l, 1])
        h = rmsnorm(x, norm_g[l, 2])
        f = conv_ffn(h, ffn_w_gate[l], ffn_w_up[l], ffn_conv_w[l], ffn_conv_b[l], ffn_w_down[l])
        x = x + rmsnorm(f, norm_g[l, 3])
    return x
```

```python
import math
from contextlib import ExitStack

import numpy as np
import ml_dtypes
import concourse.bass as bass
import concourse.mybir as mybir
from concourse.bass_utils import run_bass_kernel_spmd

F32 = mybir.dt.float32
BF16 = mybir.dt.bfloat16
I32 = mybir.dt.int32
AF = mybir.ActivationFunctionType
ALU = mybir.AluOpType
AX = mybir.AxisListType
NPBF = ml_dtypes.bfloat16

SEQ = 16384
D = 2048
DFF = 5632
NH = 16
EPS = 1e-6
ROPE_THETA = 10000.0
ATTN_SCALE = 192.0 ** -0.5
NCORE = 8
TWO_PI = 2.0 * math.pi


class KB:
    def __init__(self, nc, stack):
        self.nc = nc
        self.stack = stack
        self.engs = dict(pe=nc.tensor, dve=nc.vector, act=nc.scalar, pool=nc.gpsimd, sp=nc.sync)
        self.sems = {}
        self.cnt = {}
        for n in self.engs:
            self.sems[n] = stack.enter_context(nc.semaphore("s_" + n))
            self.cnt[n] = 0
        self.waited = {n: {} for n in self.engs}
        self.res = {}
        self.nins = 0

    def sb(self, name, shape, dt):
        return self.stack.enter_context(self.nc.sbuf_tensor("sb_" + name, list(shape), dt))

    def ps(self, name, shape, dt=F32):
        return self.stack.enter_context(self.nc.psum_tensor("ps_" + name, list(shape), dt))

    def _sem(self, key):
        if key not in self.sems:
            self.sems[key] = self.stack.enter_context(self.nc.semaphore("d_%d" % len(self.sems)))
            self.cnt[key] = 0
        return self.sems[key]

    def _R(self, r):
        if r not in self.res:
            self.res[r] = dict(w=None, r={})
        return self.res[r]

    def _waits(self, eng, reads, writes):
        need = {}

        def add(rec):
            if rec is None:
                return
            kk, v = rec
            if need.get(kk, 0) < v:
                need[kk] = v

        for r in reads:
            add(self._R(r)["w"])
        for r in writes:
            R = self._R(r)
            add(R["w"])
            for kk, v in R["r"].items():
                add((kk, v))
        e = self.engs[eng]
        for kk, v in need.items():
            if kk == "pe" and eng == "pe":
                continue
            if kk not in self.engs:
                v = self.cnt[kk]
            if self.waited[eng].get(kk, 0) >= v:
                continue
            e.wait_ge(self.sems[kk], v)
            self.waited[eng][kk] = v
            self.nins += 1

    def _update(self, rec, reads, writes):
        kk, v = rec
        for r in writes:
            R = self._R(r)
            R["w"] = rec
            R["r"] = {}
        for r in reads:
            R = self._R(r)
            if R["r"].get(kk, 0) < v:
                R["r"][kk] = v

    def op(self, eng, fn, r=(), w=()):
        self._waits(eng, r, w)
        ins = fn(self.engs[eng])
        self.cnt[eng] += 1
        ins.then_inc(self.sems[eng], 1)
        self._update((eng, self.cnt[eng]), r, w)
        self.nins += 1
        return ins

    def dma(self, q, out, in_, r=(), w=(), sem=None, **kw):
        key = ("dma", sem)
        s = self._sem(key)
        self._waits(q, r, w)
        ins = self.engs[q].dma_start(out=out, in_=in_, **kw)
        self.cnt[key] += 16
        ins.then_inc(s, 16)
        self._update((key, self.cnt[key]), r, w)
        self.nins += 1
        return ins

    def finish(self, eng="sp"):
        e = self.engs[eng]
        for key, s in self.sems.items():
            if key in self.engs or self.cnt[key] == 0:
                continue
            e.wait_ge(s, self.cnt[key])


def new_nc():
    return bass.Bass("TRN2", target_bir_lowering=False)


def din(nc, name, shape, dt):
    return nc.dram_tensor(name, list(shape), dt, kind="ExternalInput").ap()


def dout(nc, name, shape, dt):
    return nc.dram_tensor(name, list(shape), dt, kind="ExternalOutput").ap()


def dscr(nc, name, shape, dt):
    return nc.dram_tensor(name, list(shape), dt, kind="Internal").ap()


def run(nc, in_maps):
    res = run_bass_kernel_spmd(nc, in_maps, core_ids=list(range(len(in_maps))))
    return res.results


def emit_rstd(k, src, src_res, KC, ones, sq, ps, rstd, sq_res, dim, ps_res="pS", rstd_res="rstd"):
    k.op("act", lambda e: e.activation(out=sq, in_=src, func=AF.Square), r=src_res, w=sq_res)
    for c in range(KC):
        k.op("pe", lambda e: e.matmul(ps, lhsT=ones, rhs=sq[:, c, :], start=(c == 0), stop=(c == KC - 1)),
             r=sq_res + ["ones"], w=[ps_res])
    k.op("act", lambda e: e.activation(out=rstd, in_=ps, func=AF.Sqrt, scale=1.0 / dim, bias=EPS),
         r=[ps_res], w=[rstd_res])
    k.op("dve", lambda e: e.reciprocal(out=rstd, in_=rstd), r=[rstd_res], w=[rstd_res])


def bc_k(ap2, KC, N):
    return ap2.unsqueeze(2).broadcast_to([128, KC, N])


def bc_n(ap2, KC, N):
    return ap2.unsqueeze(1).broadcast_to([128, KC, N])


def emit_range_reduce(k, eng, t, ti, tf, res):
    k.op(eng, lambda e: e.tensor_scalar(out=ti, in0=t, scalar1=1.0 / TWO_PI, scalar2=None, op0=ALU.mult), r=res, w=[res[0] + "_ti"])
    k.op(eng, lambda e: e.tensor_copy(out=tf, in_=ti), r=[res[0] + "_ti"], w=[res[0] + "_tf"])
    k.op(eng, lambda e: e.scalar_tensor_tensor(out=t, in0=tf, scalar=-TWO_PI, in1=t, op0=ALU.mult, op1=ALU.add), r=res + [res[0] + "_tf"], w=res)
    k.op(eng, lambda e: e.tensor_scalar(out=tf, in0=t, scalar1=0.0, scalar2=TWO_PI, op0=ALU.is_lt, op1=ALU.mult), r=res, w=[res[0] + "_tf"])
    k.op(eng, lambda e: e.tensor_tensor(out=t, in0=t, in1=tf, op=ALU.add), r=res + [res[0] + "_tf"], w=res)
    k.op(eng, lambda e: e.tensor_scalar(out=tf, in0=t, scalar1=TWO_PI, scalar2=-TWO_PI, op0=ALU.is_ge, op1=ALU.mult), r=res, w=[res[0] + "_tf"])
    k.op(eng, lambda e: e.tensor_tensor(out=t, in0=t, in1=tf, op=ALU.add), r=res + [res[0] + "_tf"], w=res)
    k.op(eng, lambda e: e.tensor_scalar(out=t, in0=t, scalar1=-math.pi, scalar2=None, op0=ALU.add), r=res, w=res)


def build_A(TL):
    nc = new_nc()
    xT = din(nc, "xT", [D, TL], F32)
    g = din(nc, "g", [128, 16], F32)
    uT = dout(nc, "uT", [D, TL], BF16)
    xv = xT.rearrange("(k p) t -> p k t", p=128)
    uv = uT.rearrange("(k p) t -> p k t", p=128)
    NT = 256
    with ExitStack() as st:
        k = KB(nc, st)
        ones = k.sb("ones", [128, 128], BF16)
        k.op("pool", lambda e: e.memset(ones[:], 1.0), w=["ones"])
        g_sb = k.sb("g_sb", [128, 16], F32)
        k.dma("sp", g_sb[:], g, w=["g"], sem="g")
        xs = [k.sb("x%d" % i, [128, 16, NT], F32) for i in range(2)]
        us = [k.sb("u%d" % i, [128, 16, NT], BF16) for i in range(2)]
        sq = k.sb("sq", [128, 16, NT], BF16)
        tt = k.sb("tt", [128, 16, NT], F32)
        rstd = k.sb("rstd", [128, NT], F32)
        ps = k.ps("ps", [128, NT])
        for it in range(TL // NT):
            b = it % 2
            sl = slice(it * NT, (it + 1) * NT)
            k.dma("sp", xs[b][:], xv[:, :, sl], w=["x%d" % b], sem="x%d" % b)
            emit_rstd(k, xs[b][:], ["x%d" % b], 16, ones[:], sq[:], ps[:], rstd[:], ["sq"], float(D))
            k.op("dve", lambda e: e.tensor_tensor(out=tt[:], in0=xs[b][:], in1=bc_k(g_sb[:], 16, NT), op=ALU.mult), r=["x%d" % b, "g"], w=["tt"])
            k.op("dve", lambda e: e.tensor_tensor(out=us[b][:], in0=tt[:], in1=bc_n(rstd[:], 16, NT), op=ALU.mult), r=["tt", "rstd"], w=["u%d" % b])
            k.dma("pool", uv[:, :, sl], us[b][:], r=["u%d" % b], w=["out"], sem="o%d" % b)
        k.finish()
    return nc


def vec_layout(layer1):
    names = [("g1", 16), ("g2", 16), ("g3", 16), ("cw0", 44), ("cw1", 44), ("cw2", 44), ("cb", 44), ("halo", 1)]
    if not layer1:
        names += [("bv", 16), ("bg", 16), ("kvg", 16), ("qg", 16), ("ckvg", 4), ("cqg", 4)]
    off = {}
    o = 0
    for n, c in names:
        off[n] = (o, c)
        o += c
    return off, o


def ce_schedule(layer1):
    s = []
    if layer1:
        s += [("wo", 16, 128)] * 16
    else:
        s += [("glu", 16, 256)] * 16
    s += [("gu", 16, 256)] * 44
    s += [("dn", 44, 128)] * 16
    if not layer1:
        s += [("dkvc", 16, 512), ("dkvr", 16, 64), ("dq", 16, 512)]
    return s


def build_CE(TL, layer1):
    nc = new_nc()
    TLH = TL + 2
    voff, NV = vec_layout(layer1)
    sched = ce_schedule(layer1)
    offs = []
    o = 0
    for (_, KC, C) in sched:
        offs.append(o)
        o += 128 * KC * C
    NW = o
    CH = 1024 * 2048
    NWP = ((NW + CH - 1) // CH) * CH
    xT = din(nc, "xT", [D, TLH], F32)
    mixT = din(nc, "mixT", [D, TLH], BF16)
    wflat = din(nc, "wflat", [NWP], F32)
    vecs = din(nc, "vecs", [128, NV], F32)
    xoT = dout(nc, "xoT", [D, TL], F32)
    if not layer1:
        pos = din(nc, "pos", [32, TLH], I32)
        ckvT = dout(nc, "ckvT", [512, TL], BF16)
        kropeT = dout(nc, "kropeT", [64, TL], BF16)
        cqT = dout(nc, "cqT", [512, TL], BF16)
    wscr = dscr(nc, "wscr", [NWP], BF16)
    xv = xT.rearrange("(k p) t -> p k t", p=128)
    mv = mixT.rearrange("(k p) t -> p k t", p=128)
    xov = xoT.rearrange("(k p) t -> p k t", p=128)
    NT = 512
    tiles = [(0, 2)] + [(2 + i * NT, NT) for i in range(TL // NT)]
    with ExitStack() as st:
        k = KB(nc, st)
        ones = k.sb("ones", [128, 128], BF16)
        k.op("pool", lambda e: e.memset(ones[:], 1.0), w=["ones"])
        V = k.sb("vecs_sb", [128, NV], F32)
        k.dma("sp", V[:], vecs, w=["vecs"], sem="vecs")

        def vcol(name, i=None):
            o0, c = voff[name]
            if i is None:
                return V[:, o0:o0 + c]
            return V[:, o0 + i:o0 + i + 1]

        wf2 = wflat.rearrange("(r c) -> r c", c=2048)
        ws2 = wscr.rearrange("(r c) -> r c", c=2048)
        nch = NWP // CH
        for ci in range(nch):
            k.dma("pool", ws2[ci * 1024:(ci + 1) * 1024, :], wf2[ci * 1024:(ci + 1) * 1024, :], w=["wscr%d" % ci], sem="conv")

        NB = 2
        wb = [k.sb("wb%d" % i, [128, 8192], BF16) for i in range(NB)]
        allblocks = [(offs[j], sched[j][1], sched[j][2]) for _ in tiles for j in range(len(sched))]
        state = dict(issued=0, cur=0)

        def issue(j):
            o0, KC, C = allblocks[j]
            E = KC * C
            b = j % NB
            c0, c1 = o0 // CH, (o0 + 128 * E - 1) // CH
            k.dma("sp", wb[b][:, 0:E], wscr[o0:o0 + 128 * E].rearrange("(p e) -> p e", p=128),
                  r=["wscr%d" % c for c in range(c0, c1 + 1)], w=["wb%d" % b], sem="wb%d" % b)

        def wnext():
            j = state["cur"]
            while state["issued"] < min(j + NB, len(allblocks)):
                issue(state["issued"])
                state["issued"] += 1
            state["cur"] += 1
            o0, KC, C = allblocks[j]
            b = j % NB
            return wb[b][:, 0:KC * C].rearrange("p (k c) -> p k c", k=KC), "wb%d" % b

        x_sb = k.sb("x_sb", [128, 16, NT], F32)
        mix_sb = k.sb("mix_sb", [128, 16, NT], BF16)
        mT = k.sb("mT", [128, 16, NT], F32)
        aT = k.sb("aT", [128, 44, NT], BF16)
        rstd = k.sb("rstd", [128, NT], F32)
        gext = [k.sb("gext%d" % i, [128, NT + 2], F32) for i in range(2)]
        tcv = [k.sb("tcv%d" % i, [128, NT], F32) for i in range(2)]
        gl = [k.sb("gl%d" % i, [128, NT], F32) for i in range(2)]
        sig = [k.sb("sig%d" % i, [128, NT], F32) for i in range(2)]
        gprev = k.sb("gprev", [128, 44, 2], F32)
        k.op("pool", lambda e: e.memset(gprev[:], 0.0), w=["gprev%d" % f for f in range(44)])
        pA = [k.ps("pA%d" % i, [128, NT]) for i in range(2)]
        pB = [k.ps("pB%d" % i, [128, NT]) for i in range(2)]
        pS = k.ps("pS", [128, NT])
        pD = [k.ps("pD%d" % i, [128, NT]) for i in range(2)]
        sqres = ["a%d" % f for f in range(16)]
        if not layer1:
            ck = k.sb("ck", [128, 4, NT], F32)
            ckb = k.sb("ckb", [128, 4, NT], BF16)
            posi = k.sb("posi", [32, NT], I32)
            ang = k.sb("ang", [32, NT], F32)
            angi = k.sb("angi", [32, NT], I32)
            angf = k.sb("angf", [32, NT], F32)
            cs = k.sb("cs", [32, 2, NT], F32)
            kr = k.sb("kr", [32, 2, NT], BF16)
            rt = k.sb("rt", [32, 2, NT], F32)
            freq = k.sb("freq", [32, 1], F32)
            fi = k.sb("fi", [32, 1], F32)
            k.op("pool", lambda e: e.iota(fi[:], [[0, 1]], base=0, channel_multiplier=1, allow_small_or_imprecise_dtypes=True), w=["fi"])
            k.op("act", lambda e: e.activation(out=freq[:], in_=fi[:], func=AF.Exp, scale=-(2.0 / 64.0) * math.log(ROPE_THETA)), r=["fi"], w=["freq"])

        def rstd_of(src, src_res, KC, N, tag, dim):
            emit_rstd(k, src, src_res, KC, ones[:], aT[:, 0:KC, 0:N], pS[:, 0:N], rstd[:, 0:N], ["a%d" % i for i in range(KC)], dim)

        for ti, (t0, N) in enumerate(tiles):
            halo = ti == 0
            cs_ = slice(t0, t0 + N)
            k.dma("pool", x_sb[:, :, 0:N], xv[:, :, cs_], w=["x"], sem="ldx")
            k.dma("pool", mix_sb[:, :, 0:N], mv[:, :, cs_], w=["mix"], sem="ldm")
            for m in range(16):
                blk, bres = wnext()
                pa, pb = pA[m % 2][:, 0:N], pB[m % 2][:, 0:N]
                if layer1:
                    for kc in range(16):
                        k.op("pe", lambda e: e.matmul(pa, lhsT=blk[:, kc, :], rhs=mix_sb[:, kc, 0:N], start=(kc == 0), stop=(kc == 15)), r=[bres, "mix"], w=["pA%d" % (m % 2)])
                    k.op("act", lambda e: e.activation(out=mT[:, m, 0:N], in_=pa, func=AF.Copy), r=["pA%d" % (m % 2)], w=["m%d" % m])
                else:
                    for kc in range(16):
                        k.op("pe", lambda e: e.matmul(pa, lhsT=blk[:, kc, 0:128], rhs=mix_sb[:, kc, 0:N], start=(kc == 0), stop=(kc == 15)), r=[bres, "mix"], w=["pA%d" % (m % 2)])
                    for kc in range(16):
                        k.op("pe", lambda e: e.matmul(pb, lhsT=blk[:, kc, 128:256], rhs=mix_sb[:, kc, 0:N], start=(kc == 0), stop=(kc == 15)), r=[bres, "mix"], w=["pB%d" % (m % 2)])
                    sg = sig[m % 2][:, 0:N]
                    k.op("act", lambda e: e.activation(out=sg, in_=pb, func=AF.Sigmoid, bias=vcol("bg", m)), r=["pB%d" % (m % 2), "vecs"], w=["sig%d" % (m % 2)])
                    k.op("dve", lambda e: e.scalar_tensor_tensor(out=mT[:, m, 0:N], in0=pa, scalar=vcol("bv", m), in1=sg, op0=ALU.add, op1=ALU.mult),
                         r=["pA%d" % (m % 2), "sig%d" % (m % 2), "vecs"], w=["m%d" % m])
            mres = ["m%d" % m for m in range(16)]
            rstd_of(mT[:, :, 0:N], mres, 16, N, "n1", float(D))
            k.op("dve", lambda e: e.tensor_tensor(out=mT[:, :, 0:N], in0=mT[:, :, 0:N], in1=bc_k(vcol("g1"), 16, N), op=ALU.mult), r=mres + ["vecs"], w=mres)
            k.op("pool", lambda e: e.tensor_tensor(out=mT[:, :, 0:N], in0=mT[:, :, 0:N], in1=bc_n(rstd[:, 0:N], 16, N), op=ALU.mult), r=mres + ["rstd"], w=mres)
            k.op("dve", lambda e: e.tensor_tensor(out=x_sb[:, :, 0:N], in0=x_sb[:, :, 0:N], in1=mT[:, :, 0:N], op=ALU.add), r=mres + ["x"], w=["x"])
            rstd_of(x_sb[:, :, 0:N], ["x"], 16, N, "n2", float(D))
            k.op("pool", lambda e: e.tensor_tensor(out=mT[:, :, 0:N], in0=x_sb[:, :, 0:N], in1=bc_k(vcol("g2"), 16, N), op=ALU.mult), r=["x", "vecs"], w=mres)
            k.op("dve", lambda e: e.tensor_tensor(out=mix_sb[:, :, 0:N], in0=mT[:, :, 0:N], in1=bc_n(rstd[:, 0:N], 16, N), op=ALU.mult), r=mres + ["rstd"], w=["mix"])
            for f in range(44):
                blk, bres = wnext()
                pa, pb = pA[f % 2][:, 0:N], pB[f % 2][:, 0:N]
                for kc in range(16):
                    k.op("pe", lambda e: e.matmul(pa, lhsT=blk[:, kc, 0:128], rhs=mix_sb[:, kc, 0:N], start=(kc == 0), stop=(kc == 15)), r=[bres, "mix"], w=["pA%d" % (f % 2)])
                for kc in range(16):
                    k.op("pe", lambda e: e.matmul(pb, lhsT=blk[:, kc, 128:256], rhs=mix_sb[:, kc, 0:N], start=(kc == 0), stop=(kc == 15)), r=[bres, "mix"], w=["pB%d" % (f % 2)])
                gx = gext[f % 2]
                gxr = "gext%d" % (f % 2)
                k.op("pool", lambda e: e.tensor_copy(out=gx[:, 0:2], in_=gprev[:, f, :]), r=["gprev%d" % f], w=[gxr + "h"])
                k.op("act", lambda e: e.activation(out=gx[:, 2:N + 2], in_=pa, func=AF.Copy), r=["pA%d" % (f % 2)], w=[gxr])
                if halo:
                    k.op("pool", lambda e: e.tensor_scalar(out=gprev[:, f, :], in0=gx[:, N:N + 2], scalar1=vcol("halo", 0), scalar2=None, op0=ALU.mult), r=[gxr, gxr + "h", "vecs"], w=["gprev%d" % f])
                else:
                    k.op("pool", lambda e: e.tensor_copy(out=gprev[:, f, :], in_=gx[:, N:N + 2]), r=[gxr, gxr + "h"], w=["gprev%d" % f])
                tc_ = tcv[f % 2][:, 0:N]
                tcr = "tcv%d" % (f % 2)
                k.op("dve", lambda e: e.tensor_scalar(out=tc_, in0=gx[:, 2:N + 2], scalar1=vcol("cw2", f), scalar2=vcol("cb", f), op0=ALU.mult, op1=ALU.add), r=[gxr, "vecs"], w=[tcr])
                k.op("dve", lambda e: e.scalar_tensor_tensor(out=tc_, in0=gx[:, 1:N + 1], scalar=vcol("cw1", f), in1=tc_, op0=ALU.mult, op1=ALU.add), r=[gxr, gxr + "h", tcr, "vecs"], w=[tcr])
                k.op("dve", lambda e: e.scalar_tensor_tensor(out=tc_, in0=gx[:, 0:N], scalar=vcol("cw0", f), in1=tc_, op0=ALU.mult, op1=ALU.add), r=[gxr, gxr + "h", tcr, "vecs"], w=[tcr])
                g_ = gl[f % 2][:, 0:N]
                k.op("act", lambda e: e.activation(out=g_, in_=tc_, func=AF.Gelu_apprx_tanh), r=[tcr], w=["gl%d" % (f % 2)])
                k.op("dve", lambda e: e.tensor_tensor(out=aT[:, f, 0:N], in0=g_, in1=pb, op=ALU.mult), r=["gl%d" % (f % 2), "pB%d" % (f % 2)], w=["a%d" % f])
            ares = ["a%d" % f for f in range(44)]
            for m in range(16):
                blk, bres = wnext()
                pd = pD[m % 2][:, 0:N]
                for f in range(44):
                    k.op("pe", lambda e: e.matmul(pd, lhsT=blk[:, f, :], rhs=aT[:, f, 0:N], start=(f == 0), stop=(f == 43)), r=[bres, "a%d" % f], w=["pD%d" % (m % 2)])
                k.op("act", lambda e: e.activation(out=mT[:, m, 0:N], in_=pd, func=AF.Copy), r=["pD%d" % (m % 2)], w=["m%d" % m])
            rstd_of(mT[:, :, 0:N], mres, 16, N, "n3", float(D))
            k.op("dve", lambda e: e.tensor_tensor(out=mT[:, :, 0:N], in0=mT[:, :, 0:N], in1=bc_k(vcol("g3"), 16, N), op=ALU.mult), r=mres + ["vecs"], w=mres)
            k.op("pool", lambda e: e.tensor_tensor(out=mT[:, :, 0:N], in0=mT[:, :, 0:N], in1=bc_n(rstd[:, 0:N], 16, N), op=ALU.mult), r=mres + ["rstd"], w=mres)
            k.op("dve", lambda e: e.tensor_tensor(out=x_sb[:, :, 0:N], in0=x_sb[:, :, 0:N], in1=mT[:, :, 0:N], op=ALU.add), r=mres + ["x"], w=["x"])
            if halo:
                if not layer1:
                    for _ in range(3):
                        wnext()
                continue
            os_ = slice(t0 - 2, t0 - 2 + N)
            k.dma("pool", xov[:, :, os_], x_sb[:, :, 0:N], r=["x"], w=["xo"], sem="stx")
            if layer1:
                continue
            rstd_of(x_sb[:, :, 0:N], ["x"], 16, N, "n4", float(D))
            k.op("dve", lambda e: e.tensor_tensor(out=mT[:, :, 0:N], in0=x_sb[:, :, 0:N], in1=bc_n(rstd[:, 0:N], 16, N), op=ALU.mult), r=["x", "rstd"], w=mres)
            k.op("pool", lambda e: e.tensor_tensor(out=mix_sb[:, :, 0:N], in0=mT[:, :, 0:N], in1=bc_k(vcol("kvg"), 16, N), op=ALU.mult), r=mres + ["vecs"], w=["mix"])
            blk, bres = wnext()
            for mc in range(4):
                pd = pD[mc % 2][:, 0:N]
                for kc in range(16):
                    k.op("pe", lambda e: e.matmul(pd, lhsT=blk[:, kc, mc * 128:(mc + 1) * 128], rhs=mix_sb[:, kc, 0:N], start=(kc == 0), stop=(kc == 15)), r=[bres, "mix"], w=["pD%d" % (mc % 2)])
                k.op("act", lambda e: e.activation(out=ck[:, mc, 0:N], in_=pd, func=AF.Copy), r=["pD%d" % (mc % 2)], w=["ck%d" % mc])
            blk, bres = wnext()
            for kc in range(16):
                k.op("pe", lambda e: e.matmul(pA[0][0:32, 0:N], lhsT=blk[:, kc, 0:32], rhs=mix_sb[:, kc, 0:N], start=(kc == 0), stop=(kc == 15)), r=[bres, "mix"], w=["pA0"])
            for kc in range(16):
                k.op("pe", lambda e: e.matmul(pB[0][0:32, 0:N], lhsT=blk[:, kc, 32:64], rhs=mix_sb[:, kc, 0:N], start=(kc == 0), stop=(kc == 15)), r=[bres, "mix"], w=["pB0"])
            k.dma("pool", posi[:, 0:N], pos[:, cs_], w=["posi"], sem="ldp")
            k.op("dve", lambda e: e.tensor_copy(out=ang[:, 0:N], in_=posi[:, 0:N]), r=["posi"], w=["ang"])
            for j, sh in enumerate((1.5 * math.pi, math.pi)):
                k.op("dve", lambda e: e.tensor_scalar(out=rt[:, j, 0:N], in0=ang[:, 0:N], scalar1=freq[:, 0:1], scalar2=sh, op0=ALU.mult, op1=ALU.add), r=["ang", "freq"], w=["rt%d" % j])
                emit_range_reduce(k, "dve", rt[:, j, 0:N], angi[:, 0:N], angf[:, 0:N], ["rt%d" % j])
                k.op("act", lambda e: e.activation(out=cs[:, j, 0:N], in_=rt[:, j, 0:N], func=AF.Sin), r=["rt%d" % j], w=["cs%d" % j])
            A_, B_ = pA[0][0:32, 0:N], pB[0][0:32, 0:N]
            k.op("dve", lambda e: e.tensor_tensor(out=rt[:, 0, 0:N], in0=A_, in1=cs[:, 0, 0:N], op=ALU.mult), r=["pA0", "cs0"], w=["rt0"])
            k.op("dve", lambda e: e.tensor_tensor(out=rt[:, 1, 0:N], in0=B_, in1=cs[:, 1, 0:N], op=ALU.mult), r=["pB0", "cs1"], w=["rt1"])
            k.op("dve", lambda e: e.tensor_tensor(out=kr[:, 0, 0:N], in0=rt[:, 0, 0:N], in1=rt[:, 1, 0:N], op=ALU.subtract), r=["rt0", "rt1"], w=["kr0"])
            k.op("dve", lambda e: e.tensor_tensor(out=rt[:, 0, 0:N], in0=B_, in1=cs[:, 0, 0:N], op=ALU.mult), r=["pB0", "cs0", "kr0"], w=["rt0"])
            k.op("dve", lambda e: e.tensor_tensor(out=rt[:, 1, 0:N], in0=A_, in1=cs[:, 1, 0:N], op=ALU.mult), r=["pA0", "cs1", "kr0"], w=["rt1"])
            k.op("dve", lambda e: e.tensor_tensor(out=kr[:, 1, 0:N], in0=rt[:, 0, 0:N], in1=rt[:, 1, 0:N], op=ALU.add), r=["rt0", "rt1"], w=["kr1"])
            k.dma("pool", kropeT[0:32, os_], kr[:, 0, 0:N], r=["kr0"], w=["kro"], sem="stk")
            k.dma("pool", kropeT[32:64, os_], kr[:, 1, 0:N], r=["kr1"], w=["kro"], sem="stk")
            ckres = ["ck%d" % i for i in range(4)]
            rstd_of(ck[:, :, 0:N], ckres, 4, N, "n5", 512.0)
            k.op("dve", lambda e: e.tensor_tensor(out=ck[:, :, 0:N], in0=ck[:, :, 0:N], in1=bc_k(vcol("ckvg"), 4, N), op=ALU.mult), r=ckres + ["vecs"], w=ckres)
            k.op("dve", lambda e: e.tensor_tensor(out=ckb[:, :, 0:N], in0=ck[:, :, 0:N], in1=bc_n(rstd[:, 0:N], 4, N), op=ALU.mult), r=ckres + ["rstd"], w=["ckb"])
            k.dma("pool", ckvT.rearrange("(k p) t -> p k t", p=128)[:, :, os_], ckb[:, :, 0:N], r=["ckb"], w=["ckvo"], sem="stc")
            k.op("pool", lambda e: e.tensor_tensor(out=mix_sb[:, :, 0:N], in0=mT[:, :, 0:N], in1=bc_k(vcol("qg"), 16, N), op=ALU.mult), r=mres + ["vecs"], w=["mix"])
            blk, bres = wnext()
            for mc in range(4):
                pd = pD[mc % 2][:, 0:N]
                for kc in range(16):
                    k.op("pe", lambda e: e.matmul(pd, lhsT=blk[:, kc, mc * 128:(mc + 1) * 128], rhs=mix_sb[:, kc, 0:N], start=(kc == 0), stop=(kc == 15)), r=[bres, "mix"], w=["pD%d" % (mc % 2)])
                k.op("act", lambda e: e.activation(out=ck[:, mc, 0:N], in_=pd, func=AF.Copy), r=["pD%d" % (mc % 2)], w=["ck%d" % mc])
            rstd_of(ck[:, :, 0:N], ckres, 4, N, "n6", 512.0)
            k.op("dve", lambda e: e.tensor_tensor(out=ck[:, :, 0:N], in0=ck[:, :, 0:N], in1=bc_k(vcol("cqg"), 4, N), op=ALU.mult), r=ckres + ["vecs"], w=ckres)
            k.op("dve", lambda e: e.tensor_tensor(out=ckb[:, :, 0:N], in0=ck[:, :, 0:N], in1=bc_n(rstd[:, 0:N], 4, N), op=ALU.mult), r=ckres + ["rstd"], w=["ckb"])
            k.dma("pool", cqT.rearrange("(k p) t -> p k t", p=128)[:, :, os_], ckb[:, :, 0:N], r=["ckb"], w=["cqo"], sem="stq")
        k.finish()
    return nc


def wblock(W, cols):
    K_ = W.shape[0]
    return np.ascontiguousarray(W[:, cols].reshape(K_ // 128, 128, len(cols)).transpose(1, 0, 2))


def pcol(v):
    return np.ascontiguousarray(v.reshape(-1, 128).T)


def pack_CE(layer1, W, vec):
    blocks = []
    ar = np.arange
    if layer1:
        for m in range(16):
            blocks.append(wblock(W["wo"], ar(m * 128, (m + 1) * 128)))
    else:
        for m in range(16):
            blocks.append(wblock(W["glu"], np.concatenate([ar(m * 128, (m + 1) * 128), 2048 + ar(m * 128, (m + 1) * 128)])))
    for f in range(44):
        c = ar(f * 128, (f + 1) * 128)
        blocks.append(np.concatenate([wblock(W["gate"], c), wblock(W["up"], c)], axis=2))
    for m in range(16):
        blocks.append(wblock(W["down"], ar(m * 128, (m + 1) * 128)))
    if not layer1:
        blocks.append(wblock(W["dkv"], ar(0, 512)))
        blocks.append(wblock(W["dkv"], ar(512, 576)))
        blocks.append(wblock(W["dq"], ar(0, 512)))
    flat = np.concatenate([b.reshape(-1) for b in blocks]).astype(np.float32)
    CH = 1024 * 2048
    NWP = ((flat.size + CH - 1) // CH) * CH
    out = np.zeros(NWP, np.float32)
    out[:flat.size] = flat
    voff, NV = vec_layout(layer1)
    vecs = np.zeros((128, NV), np.float32)
    for n, (o0, c) in voff.items():
        if n == "halo":
            continue
        vecs[:, o0:o0 + c] = vec[n]
    return out, vecs, voff


def build_P(GP, NCH):
    nc = new_nc()
    are = din(nc, "are", [128, GP], F32)
    aim = din(nc, "aim", [128, GP], F32)
    ldt = din(nc, "ldt", [128, GP], F32)
    bre = din(nc, "bre", [128, GP, 16], F32)
    bim = din(nc, "bim", [128, GP, 16], F32)
    cre = din(nc, "cre", [128, GP, 16], F32)
    cim = din(nc, "cim", [128, GP, 16], F32)
    AB1 = dout(nc, "AB1", [GP, 128, 2048], BF16)
    AB2 = dout(nc, "AB2", [GP, 128, 2048], BF16)
    CAV = dout(nc, "CAV", [GP, 128, 2048], BF16)
    KT = dout(nc, "KT", [GP, 128, 256], BF16)
    SC = dout(nc, "SC", [GP, 5, 128, NCH], F32)
    with ExitStack() as st:
        k = KB(nc, st)
        t_ = {}
        for n, src, shp in (("are", are, [128, GP]), ("aim", aim, [128, GP]), ("ldt", ldt, [128, GP]),
                            ("bre", bre, [128, GP, 16]), ("bim", bim, [128, GP, 16]), ("cre", cre, [128, GP, 16]), ("cim", cim, [128, GP, 16])):
            t_[n] = k.sb(n, shp, F32)
            k.dma("sp", t_[n][:], src, w=[n], sem="in")
        S = lambda n, shp=(128, GP): k.sb(n, list(shp), F32)
        dt_, dre, dim_, mag = S("dt"), S("dre"), S("dim"), S("mag")
        tc_, ts_, ti_ = S("tc"), S("ts"), k.sb("ti", [128, GP], I32)
        tf_ = S("tf")
        abre, abim, den, nr, cr, ci, tmp, tmp2 = S("abre"), S("abim"), S("den"), S("nr"), S("cr"), S("ci"), S("tmp"), S("tmp2")
        dre128, phi128 = S("dre128"), S("phi128")
        V = "dve"
        k.op("act", lambda e: e.activation(out=dt_[:], in_=t_["ldt"][:], func=AF.Exp), r=["ldt"], w=["dt"])
        k.op(V, lambda e: e.tensor_tensor(out=dre[:], in0=dt_[:], in1=t_["are"][:], op=ALU.mult), r=["dt", "are"], w=["dre"])
        k.op(V, lambda e: e.tensor_tensor(out=dim_[:], in0=dt_[:], in1=t_["aim"][:], op=ALU.mult), r=["dt", "aim"], w=["dim"])
        k.op("act", lambda e: e.activation(out=mag[:], in_=dre[:], func=AF.Exp), r=["dre"], w=["mag"])
        k.op(V, lambda e: e.tensor_scalar(out=tc_[:], in0=dim_[:], scalar1=1.5 * math.pi + 4 * math.pi, scalar2=None, op0=ALU.add), r=["dim"], w=["tc"])
        k.op(V, lambda e: e.tensor_scalar(out=ts_[:], in0=dim_[:], scalar1=math.pi + 4 * math.pi, scalar2=None, op0=ALU.add), r=["dim"], w=["ts"])
        emit_range_reduce(k, V, tc_[:], ti_[:], tf_[:], ["tc"])
        emit_range_reduce(k, V, ts_[:], ti_[:], tf_[:], ["ts"])
        k.op("act", lambda e: e.activation(out=tc_[:], in_=tc_[:], func=AF.Sin), r=["tc"], w=["tc"])
        k.op("act", lambda e: e.activation(out=ts_[:], in_=ts_[:], func=AF.Sin), r=["ts"], w=["ts"])
        k.op(V, lambda e: e.tensor_tensor(out=abre[:], in0=mag[:], in1=tc_[:], op=ALU.mult), r=["mag", "tc"], w=["abre"])
        k.op(V, lambda e: e.tensor_tensor(out=abim[:], in0=mag[:], in1=ts_[:], op=ALU.mult), r=["mag", "ts"], w=["abim"])
        k.op(V, lambda e: e.tensor_tensor(out=den[:], in0=t_["are"][:], in1=t_["are"][:], op=ALU.mult), r=["are"], w=["den"])
        k.op(V, lambda e: e.tensor_tensor(out=tmp[:], in0=t_["aim"][:], in1=t_["aim"][:], op=ALU.mult), r=["aim"], w=["tmp"])
        k.op(V, lambda e: e.tensor_tensor(out=den[:], in0=den[:], in1=tmp[:], op=ALU.add), r=["den", "tmp"], w=["den"])
        k.op(V, lambda e: e.reciprocal(out=den[:], in_=den[:]), r=["den"], w=["den"])
        k.op(V, lambda e: e.tensor_scalar(out=nr[:], in0=abre[:], scalar1=-1.0, scalar2=None, op0=ALU.add), r=["abre"], w=["nr"])
        k.op(V, lambda e: e.tensor_tensor(out=tmp[:], in0=nr[:], in1=t_["are"][:], op=ALU.mult), r=["nr", "are", "den"], w=["tmp"])
        k.op(V, lambda e: e.tensor_tensor(out=tmp2[:], in0=abim[:], in1=t_["aim"][:], op=ALU.mult), r=["abim", "aim"], w=["tmp2"])
        k.op(V, lambda e: e.tensor_tensor(out=tmp[:], in0=tmp[:], in1=tmp2[:], op=ALU.add), r=["tmp", "tmp2"], w=["tmp"])
        k.op(V, lambda e: e.tensor_tensor(out=cr[:], in0=tmp[:], in1=den[:], op=ALU.mult), r=["tmp", "den"], w=["cr"])
        k.op(V, lambda e: e.tensor_tensor(out=tmp[:], in0=abim[:], in1=t_["are"][:], op=ALU.mult), r=["abim", "are", "cr"], w=["tmp"])
        k.op(V, lambda e: e.tensor_tensor(out=tmp2[:], in0=nr[:], in1=t_["aim"][:], op=ALU.mult), r=["nr", "aim", "cr"], w=["tmp2"])
        k.op(V, lambda e: e.tensor_tensor(out=tmp[:], in0=tmp[:], in1=tmp2[:], op=ALU.subtract), r=["tmp", "tmp2"], w=["tmp"])
        k.op(V, lambda e: e.tensor_tensor(out=ci[:], in0=tmp[:], in1=den[:], op=ALU.mult), r=["tmp", "den"], w=["ci"])
        k.op(V, lambda e: e.tensor_scalar(out=dre128[:], in0=dre[:], scalar1=128.0, scalar2=None, op0=ALU.mult), r=["dre"], w=["dre128"])
        k.op(V, lambda e: e.tensor_scalar(out=phi128[:], in0=dim_[:], scalar1=128.0, scalar2=None, op0=ALU.mult), r=["dim"], w=["phi128"])
        bbre, bbim, b1, b2 = S("bbre", (128, GP, 16)), S("bbim", (128, GP, 16)), S("b1", (128, GP, 16)), S("b2", (128, GP, 16))
        bbst = S("bbst", (128, GP, 16))
        cb_ = lambda a: a[:].unsqueeze(2).broadcast_to([128, GP, 16])
        k.op(V, lambda e: e.tensor_tensor(out=b1[:], in0=t_["bre"][:], in1=cb_(cr), op=ALU.mult), r=["bre", "cr"], w=["b1"])
        k.op(V, lambda e: e.tensor_tensor(out=b2[:], in0=t_["bim"][:], in1=cb_(ci), op=ALU.mult), r=["bim", "ci"], w=["b2"])
        k.op(V, lambda e: e.tensor_tensor(out=bbre[:], in0=b1[:], in1=b2[:], op=ALU.subtract), r=["b1", "b2"], w=["bbre"])
        k.op(V, lambda e: e.tensor_tensor(out=b1[:], in0=t_["bim"][:], in1=cb_(cr), op=ALU.mult), r=["bim", "cr", "bbre"], w=["b1"])
        k.op(V, lambda e: e.tensor_tensor(out=b2[:], in0=t_["bre"][:], in1=cb_(ci), op=ALU.mult), r=["bre", "ci", "bbre"], w=["b2"])
        k.op(V, lambda e: e.tensor_tensor(out=bbim[:], in0=b1[:], in1=b2[:], op=ALU.add), r=["b1", "b2"], w=["bbim"])
        k.op(V, lambda e: e.tensor_copy(out=bbst[0:64], in_=bbre[0:64]), r=["bbre"], w=["bbst_a"])
        k.op(V, lambda e: e.tensor_copy(out=bbst[64:128], in_=bbim[64:128]), r=["bbim"], w=["bbst_b"])
        tau = S("tau", (128, 129))
        cidx = S("cidx", (128, NCH))
        k.op("pool", lambda e: e.iota(tau[:], [[1, 129]], base=0, channel_multiplier=0, allow_small_or_imprecise_dtypes=True), w=["tau"])
        k.op("pool", lambda e: e.iota(cidx[:], [[1, NCH]], base=0, channel_multiplier=0, allow_small_or_imprecise_dtypes=True), w=["cidx"])
        NE = 129
        ac, as_ = S("ac", (128, NE)), S("as", (128, NE))
        ai, af = k.sb("ai", [128, NE], I32), S("af", (128, NE))
        Em, Ere, Eim = S("Em", (128, NE)), S("Ere", (128, NE)), S("Eim", (128, NE))
        t1, t2 = S("t1", (128, NE, 16)), S("t2", (128, NE, 16))
        ABre, ABim, nABre = [k.sb(n, [128, 128, 16], BF16) for n in ("ABre", "ABim", "nABre")]
        CAst = S("CAst", (128, NE, 16))
        CAb16 = k.sb("CAb16", [128, 128 * 16], BF16)
        ksb = k.sb("ksb", [128, 256], BF16)
        pK = k.ps("pK", [128, 256])
        sc_, ss_ = S("sc", (128, NCH)), S("ss", (128, NCH))
        sci, scf = k.sb("sci", [128, NCH], I32), S("scf", (128, NCH))
        cosT, sinT, ncos, nsin, Rt = S("cosT", (128, NCH)), S("sinT", (128, NCH)), S("ncos", (128, NCH)), S("nsin", (128, NCH)), S("Rt", (128, NCH))
        bE = lambda a, n: a[:, 0:n].unsqueeze(2).broadcast_to([128, n, 16])
        bP = lambda a, g, n: a[:, g, :].unsqueeze(1).broadcast_to([128, n, 16])
        for g in range(GP):
            gs = slice(g, g + 1)
            k.op(V, lambda e: e.tensor_scalar(out=ac[:], in0=tau[:], scalar1=dim_[:, gs], scalar2=1.5 * math.pi + 4 * math.pi, op0=ALU.mult, op1=ALU.add), r=["tau", "dim"], w=["ac"])
            k.op(V, lambda e: e.tensor_scalar(out=as_[:], in0=tau[:], scalar1=dim_[:, gs], scalar2=math.pi + 4 * math.pi, op0=ALU.mult, op1=ALU.add), r=["tau", "dim"], w=["as"])
            emit_range_reduce(k, V, ac[:], ai[:], af[:], ["ac"])
            emit_range_reduce(k, V, as_[:], ai[:], af[:], ["as"])
            k.op("act", lambda e: e.activation(out=ac[:], in_=ac[:], func=AF.Sin), r=["ac"], w=["ac"])
            k.op("act", lambda e: e.activation(out=as_[:], in_=as_[:], func=AF.Sin), r=["as"], w=["as"])
            k.op("act", lambda e: e.activation(out=Em[:], in_=tau[:], func=AF.Exp, scale=dre[:, gs]), r=["tau", "dre"], w=["Em"])
            k.op(V, lambda e: e.tensor_tensor(out=Ere[:], in0=Em[:], in1=ac[:], op=ALU.mult), r=["Em", "ac"], w=["Ere"])
            k.op(V, lambda e: e.tensor_tensor(out=Eim[:], in0=Em[:], in1=as_[:], op=ALU.mult), r=["Em", "as"], w=["Eim"])
            k.op(V, lambda e: e.tensor_tensor(out=t1[:, 0:128, :], in0=bE(Ere, 128), in1=bP(bbre, g, 128), op=ALU.mult), r=["Ere", "bbre"], w=["t1"])
            k.op("pool", lambda e: e.tensor_tensor(out=t2[:, 0:128, :], in0=bE(Eim, 128), in1=bP(bbim, g, 128), op=ALU.mult), r=["Eim", "bbim"], w=["t2"])
            k.op(V, lambda e: e.tensor_tensor(out=ABre[:], in0=t1[:, 0:128, :], in1=t2[:, 0:128, :], op=ALU.subtract), r=["t1", "t2"], w=["ABre"])
            k.op("pool", lambda e: e.tensor_scalar(out=nABre[:], in0=ABre[:], scalar1=-1.0, scalar2=None, op0=ALU.mult), r=["ABre"], w=["nABre"])
            k.op(V, lambda e: e.tensor_tensor(out=t1[:, 0:128, :], in0=bE(Ere, 128), in1=bP(bbim, g, 128), op=ALU.mult), r=["Ere", "bbim", "ABre"], w=["t1"])
            k.op("pool", lambda e: e.tensor_tensor(out=t2[:, 0:128, :], in0=bE(Eim, 128), in1=bP(bbre, g, 128), op=ALU.mult), r=["Eim", "bbre", "ABre"], w=["t2"])
            k.op(V, lambda e: e.tensor_tensor(out=ABim[:], in0=t1[:, 0:128, :], in1=t2[:, 0:128, :], op=ALU.add), r=["t1", "t2"], w=["ABim"])
            fl = lambda a: a[:].rearrange("p t c -> p (t c)")
            k.dma("sp", AB1[g, 0:64, :], fl(ABre)[0:64], r=["ABre"], w=["o1"], sem="o1")
            k.dma("sp", AB1[g, 64:128, :], fl(ABim)[64:128], r=["ABim"], w=["o1"], sem="o1")
            k.dma("sp", AB2[g, 0:64, :], fl(ABim)[0:64], r=["ABim"], w=["o2"], sem="o2")
            k.dma("sp", AB2[g, 64:128, :], fl(nABre)[64:128], r=["nABre"], w=["o2"], sem="o2")
            k.op(V, lambda e: e.tensor_tensor(out=t1[:], in0=bE(Ere, NE), in1=bP(t_["cre"], g, NE), op=ALU.mult), r=["Ere", "cre", "ABim"], w=["t1"])
            k.op("pool", lambda e: e.tensor_tensor(out=t2[:], in0=bE(Eim, NE), in1=bP(t_["cim"], g, NE), op=ALU.mult), r=["Eim", "cim", "ABim"], w=["t2"])
            k.op(V, lambda e: e.tensor_tensor(out=CAst[0:64], in0=t1[0:64], in1=t2[0:64], op=ALU.subtract), r=["t1", "t2"], w=["CAa"])
            k.op(V, lambda e: e.tensor_tensor(out=t1[:], in0=bE(Ere, NE), in1=bP(t_["cim"], g, NE), op=ALU.mult), r=["Ere", "cim", "CAa"], w=["t1"])
            k.op("pool", lambda e: e.tensor_tensor(out=t2[:], in0=bE(Eim, NE), in1=bP(t_["cre"], g, NE), op=ALU.mult), r=["Eim", "cre", "CAa"], w=["t2"])
            k.op(V, lambda e: e.scalar_tensor_tensor(out=CAst[64:128], in0=t1[64:128], scalar=-1.0, in1=t2[64:128], op0=ALU.mult, op1=ALU.subtract), r=["t1", "t2"], w=["CAb"])
            CAf = CAst[:].rearrange("p t c -> p (t c)")
            k.op("act", lambda e: e.activation(out=CAb16[:], in_=CAf[:, 16:16 + 2048], func=AF.Copy), r=["CAa", "CAb"], w=["CAb16"])
            k.dma("sp", CAV[g], CAb16[:], r=["CAb16"], w=["o3"], sem="o3")
            for blk in range(16):
                k.op("pe", lambda e: e.matmul(pK[:, blk * 16:(blk + 1) * 16], lhsT=CAf[:, blk * 128:(blk + 1) * 128], rhs=bbst[:, g, :], start=True, stop=True),
                     r=["CAa", "CAb", "bbst_a", "bbst_b"], w=["pK"])
            k.op("act", lambda e: e.activation(out=ksb[:], in_=pK[:], func=AF.Copy), r=["pK"], w=["ksb"])
            k.dma("sp", KT[g], ksb[:], r=["ksb"], w=["o4"], sem="o4")
            k.op(V, lambda e: e.tensor_scalar(out=sc_[:], in0=cidx[:], scalar1=phi128[:, gs], scalar2=1.5 * math.pi + 4 * math.pi, op0=ALU.mult, op1=ALU.add), r=["cidx", "phi128"], w=["sc"])
            k.op(V, lambda e: e.tensor_scalar(out=ss_[:], in0=cidx[:], scalar1=phi128[:, gs], scalar2=math.pi + 4 * math.pi, op0=ALU.mult, op1=ALU.add), r=["cidx", "phi128"], w=["ss"])
            emit_range_reduce(k, V, sc_[:], sci[:], scf[:], ["sc"])
            emit_range_reduce(k, V, ss_[:], sci[:], scf[:], ["ss"])
            k.op("act", lambda e: e.activation(out=cosT[:], in_=sc_[:], func=AF.Sin), r=["sc"], w=["cosT"])
            k.op("act", lambda e: e.activation(out=sinT[:], in_=ss_[:], func=AF.Sin), r=["ss"], w=["sinT"])
            k.op("pool", lambda e: e.tensor_scalar(out=ncos[:], in0=cosT[:], scalar1=-1.0, scalar2=None, op0=ALU.mult), r=["cosT"], w=["ncos"])
            k.op("pool", lambda e: e.tensor_scalar(out=nsin[:], in0=sinT[:], scalar1=-1.0, scalar2=None, op0=ALU.mult), r=["sinT"], w=["nsin"])
            k.op("act", lambda e: e.activation(out=Rt[:], in_=cidx[:], func=AF.Exp, scale=0.0, bias=dre128[:, gs]), r=["cidx", "dre128"], w=["Rt"])
            k.dma("sp", SC[g, 0], Rt[:], r=["Rt"], w=["o5"], sem="o5")
            k.dma("sp", SC[g, 1], cosT[:], r=["cosT"], w=["o5"], sem="o5")
            k.dma("sp", SC[g, 2], sinT[:], r=["sinT"], w=["o5"], sem="o5")
            k.dma("sp", SC[g, 3, 0:64], cosT[0:64], r=["cosT"], w=["o5"], sem="o5")
            k.dma("sp", SC[g, 3, 64:128], ncos[64:128], r=["ncos"], w=["o5"], sem="o5")
            k.dma("sp", SC[g, 4, 0:64], nsin[0:64], r=["nsin"], w=["o5"], sem="o5")
            k.dma("sp", SC[g, 4, 64:128], sinT[64:128], r=["sinT"], w=["o5"], sem="o5")
        k.finish()
    return nc


def build_S(GS, NCH):
    nc = new_nc()
    U = din(nc, "U", [GS, 128, 16 * NCH], BF16)
    TB = din(nc, "TB", [GS, 128, 2048], BF16)
    W1 = din(nc, "W1", [GS, 128, 2048], BF16)
    W2 = din(nc, "W2", [GS, 128, 2048], BF16)
    VT = din(nc, "VT", [GS, 128, 2048], BF16)
    SC = din(nc, "SC", [GS, 5, 128, NCH], F32)
    Dv = din(nc, "Dv", [128, GS], F32)
    YG = dout(nc, "YG", [GS, 128, 16 * NCH], BF16)
    with ExitStack() as st:
        k = KB(nc, st)
        dv = k.sb("dv", [128, GS], F32)
        k.dma("sp", dv[:], Dv, w=["dv"], sem="dv")
        u_ = [k.sb("u%d" % i, [128, 16, NCH], BF16) for i in range(2)]
        tb_ = [k.sb("tb%d" % i, [128, 16, 128], BF16) for i in range(2)]
        w1_ = [k.sb("w1%d" % i, [128, 16, 128], BF16) for i in range(2)]
        w2_ = [k.sb("w2%d" % i, [128, 16, 128], BF16) for i in range(2)]
        vt_ = [k.sb("vt%d" % i, [128, 16, 128], BF16) for i in range(2)]
        sc_ = [k.sb("sc%d" % i, [128, 5, NCH], F32) for i in range(2)]
        yo_ = [k.sb("yo%d" % i, [128, 16, NCH], BF16) for i in range(2)]
        ysb = [k.sb("ysb%d" % i, [128, 4, NCH], F32) for i in range(2)]
        xt, xp, t2, G, Gp, H = [k.sb(n, [128, NCH], F32) for n in ("xt", "xp", "t2", "G", "Gp", "H")]
        hprev = k.sb("hprev", [128, NCH], BF16)
        k.op("pool", lambda e: e.memset(hprev[:], 0.0), w=["hprev"])
        pX = k.ps("pX", [128, 2, NCH])
        pY = [k.ps("pY%d" % i, [128, 4, NCH]) for i in range(4)]
        for g in range(GS):
            b = g % 2
            sfx = str(b)
            k.dma("sp", u_[b][:].rearrange("p i c -> p (i c)"), U[g], w=["u" + sfx], sem="u" + sfx)
            k.dma("sp", w1_[b][:].rearrange("p i c -> p (i c)"), W1[g], w=["w1" + sfx], sem="t" + sfx)
            k.dma("sp", w2_[b][:].rearrange("p i c -> p (i c)"), W2[g], w=["w2" + sfx], sem="t" + sfx)
            k.dma("sp", tb_[b][:].rearrange("p i c -> p (i c)"), TB[g], w=["tb" + sfx], sem="t" + sfx)
            k.dma("sp", vt_[b][:].rearrange("p i c -> p (i c)"), VT[g], w=["vt" + sfx], sem="t" + sfx)
            k.dma("sp", sc_[b][:], SC[g].rearrange("f p c -> p f c"), w=["sc" + sfx], sem="s" + sfx)
            ug, scg = u_[b], sc_[b]
            for i in range(16):
                k.op("pe", lambda e: e.matmul(pX[:, 0, :], lhsT=w1_[b][:, i, :], rhs=ug[:, i, :], start=(i == 0), stop=(i == 15)), r=["w1" + sfx, "u" + sfx], w=["pX"])
            for i in range(16):
                k.op("pe", lambda e: e.matmul(pX[:, 1, :], lhsT=w2_[b][:, i, :], rhs=ug[:, i, :], start=(i == 0), stop=(i == 15)), r=["w2" + sfx, "u" + sfx], w=["pX"])
            R_, C2, S2, C2p, S2p = (scg[:, j, :] for j in range(5))
            V = "dve"
            k.op(V, lambda e: e.tensor_tensor(out=xt[:], in0=pX[:, 0, :], in1=C2, op=ALU.mult), r=["pX", "sc" + sfx], w=["xt"])
            k.op(V, lambda e: e.tensor_tensor(out=t2[:], in0=pX[:, 1, :], in1=S2, op=ALU.mult), r=["pX", "sc" + sfx], w=["t2"])
            k.op(V, lambda e: e.tensor_tensor(out=xt[:], in0=xt[:], in1=t2[:], op=ALU.add), r=["xt", "t2"], w=["xt"])
            k.op(V, lambda e: e.tensor_tensor(out=xp[:], in0=pX[:, 0, :], in1=S2p, op=ALU.mult), r=["pX", "sc" + sfx], w=["xp"])
            k.op(V, lambda e: e.tensor_tensor(out=t2[:], in0=pX[:, 1, :], in1=C2p, op=ALU.mult), r=["pX", "sc" + sfx, "xt"], w=["t2"])
            k.op(V, lambda e: e.tensor_tensor(out=xp[:], in0=xp[:], in1=t2[:], op=ALU.add), r=["xp", "t2"], w=["xp"])
            k.op(V, lambda e: e.tensor_tensor_scan(out=G[:], data0=R_, data1=xt[:], initial=0.0, op0=ALU.mult, op1=ALU.add), r=["xt", "sc" + sfx], w=["G"])
            k.op(V, lambda e: e.tensor_tensor_scan(out=Gp[:], data0=R_, data1=xp[:], initial=0.0, op0=ALU.mult, op1=ALU.add), r=["xp", "sc" + sfx], w=["Gp"])
            k.op(V, lambda e: e.tensor_tensor(out=H[:], in0=G[:], in1=C2, op=ALU.mult), r=["G", "sc" + sfx], w=["H"])
            k.op(V, lambda e: e.tensor_tensor(out=t2[:], in0=Gp[:], in1=S2p, op=ALU.mult), r=["Gp", "sc" + sfx, "xp"], w=["t2"])
            if NCH > 1:
                k.op(V, lambda e: e.tensor_tensor(out=hprev[:, 1:NCH], in0=H[:, 0:NCH - 1], in1=t2[:, 0:NCH - 1], op=ALU.add), r=["H", "t2"], w=["hprev"])
            for i in range(16):
                bank = i // 4
                reg = pY[bank][:, i % 4, :]
                for ip in range(i + 1):
                    k.op("pe", lambda e: e.matmul(reg, lhsT=tb_[b][:, i - ip, :], rhs=ug[:, ip, :], start=(ip == 0), stop=False), r=["tb" + sfx, "u" + sfx], w=["pY%d" % bank])
                k.op("pe", lambda e: e.matmul(reg, lhsT=vt_[b][:, i, :], rhs=hprev[:], start=False, stop=True), r=["vt" + sfx, "hprev"], w=["pY%d" % bank])
                if i % 4 == 3:
                    ys = ysb[bank % 2]
                    k.op(V, lambda e: e.scalar_tensor_tensor(out=ys[:], in0=ug[:, 4 * bank:4 * bank + 4, :], scalar=dv[:, g:g + 1], in1=pY[bank][:], op0=ALU.mult, op1=ALU.add),
                         r=["u" + sfx, "dv", "pY%d" % bank], w=["ysb%d" % (bank % 2)])
                    k.op("act", lambda e: e.activation(out=yo_[b][:, 4 * bank:4 * bank + 4, :], in_=ys[:], func=AF.Gelu_apprx_tanh), r=["ysb%d" % (bank % 2)], w=["yo" + sfx])
            k.dma("pool", YG[g], yo_[b][:].rearrange("p i c -> p (i c)"), r=["yo" + sfx], w=["out"], sem="o" + sfx)
        k.finish()
    return nc


_NC_CACHE = {}


def get_nc(name, fn, *args):
    key = (name,) + args
    if key not in _NC_CACHE:
        _NC_CACHE[key] = fn(*args)
    return _NC_CACHE[key]


def dup(a):
    return np.ascontiguousarray(np.concatenate([a, a], axis=0))


def s5_tables(a_re, a_im, log_dt, b_re, b_im, c_re, c_im, NCH):
    GP = 16
    maps = []
    for c in range(NCORE):
        gs = slice(c * GP, (c + 1) * GP)
        maps.append(dict(
            are=dup(a_re[gs].T), aim=dup(a_im[gs].T),
            ldt=np.ascontiguousarray(np.broadcast_to(log_dt[gs][None, :], (128, GP))),
            bre=dup(b_re[gs].transpose(1, 0, 2)), bim=dup(b_im[gs].transpose(1, 0, 2)),
            cre=dup(c_re[gs].transpose(2, 0, 1)), cim=dup(c_im[gs].transpose(2, 0, 1))))
    res = run(get_nc("P", build_P, GP, NCH), maps)
    AB1 = np.concatenate([r["AB1"] for r in res], 0)
    AB2 = np.concatenate([r["AB2"] for r in res], 0)
    CAV = np.concatenate([r["CAV"] for r in res], 0)
    KT = np.concatenate([r["KT"] for r in res], 0)
    SC = np.concatenate([r["SC"] for r in res], 0)
    G = AB1.shape[0]

    def wexp(AB):
        A = AB.reshape(G, 128, 128, 16)[:, :, ::-1, :].reshape(G, 128, 16, 8, 16)
        return np.ascontiguousarray(A.transpose(0, 3, 4, 2, 1)).reshape(G, 128, 2048)

    W1, W2 = wexp(AB1), wexp(AB2)
    Karr = KT.reshape(G, 8, 16, 16, 16).transpose(0, 3, 1, 2, 4).reshape(G, 128, 16, 16)
    Kpad = np.concatenate([np.zeros((G, 7, 16, 16), Karr.dtype), Karr], axis=1)
    bb, ss, tt = np.meshgrid(np.arange(16), np.arange(8), np.arange(8), indexing="ij")
    idx = 8 * bb + tt - ss + 7
    TBf = Kpad[:, idx]
    TB = np.ascontiguousarray(TBf.transpose(0, 2, 5, 1, 3, 4)).reshape(G, 128, 2048)
    return dict(W1=W1, W2=W2, VT=np.ascontiguousarray(CAV), TB=TB, SC=SC)


def s5_run(uT, tabs, d_skip):
    B, _, S = uT.shape
    NCH = S // 128
    GS = 128 // (NCORE // B)
    maps = []
    for c in range(NCORE):
        b, q = divmod(c, NCORE // B)
        gs = slice(q * GS, (q + 1) * GS)
        ug = uT[b, q * GS * 16:(q + 1) * GS * 16, :].reshape(GS, 16, NCH, 16, 8)
        Uc = np.ascontiguousarray(ug.transpose(0, 4, 1, 3, 2)).reshape(GS, 128, 16 * NCH)
        Dv = np.ascontiguousarray(np.tile(d_skip[gs].T, (8, 1)))
        maps.append(dict(U=Uc, TB=tabs["TB"][gs], W1=tabs["W1"][gs], W2=tabs["W2"][gs], VT=tabs["VT"][gs],
                         SC=np.ascontiguousarray(tabs["SC"][gs]), Dv=Dv))
    res = run(get_nc("S", build_S, GS, NCH), maps)
    out = np.empty((B, 2048, S), NPBF)
    for c in range(NCORE):
        b, q = divmod(c, NCORE // B)
        Y = res[c]["YG"].reshape(GS, 8, 16, 16, NCH)
        out[b, q * GS * 16:(q + 1) * GS * 16, :] = Y.transpose(0, 2, 4, 3, 1).reshape(GS * 16, S)
    return out


def build_D(S, HG):
    nc = new_nc()
    cqT = din(nc, "cqT", [512, S], BF16)
    ckvT = din(nc, "ckvT", [512, S], BF16)
    kropeT = din(nc, "kropeT", [64, S], BF16)
    pos = din(nc, "pos", [32, S], I32)
    wuq = din(nc, "wuq", [128, 4 * HG * 192], F32)
    wukv = din(nc, "wukv", [128, 4 * HG * 256], F32)
    attnT = dout(nc, "attnT", [HG * 128, S], BF16)
    csd = dscr(nc, "csd", [32, 2, S], F32)
    cqv = cqT.rearrange("(k p) t -> p k t", p=128)
    ckv = ckvT.rearrange("(k p) t -> p k t", p=128)
    NT = 512
    NQT = S // NT
    NKT = S // 128
    with ExitStack() as st:
        k = KB(nc, st)
        V = "dve"
        ones = k.sb("ones", [128, 128], BF16)
        k.op("pool", lambda e: e.memset(ones[:], 1.0), w=["ones"])
        wq = k.sb("wq", [128, 4, HG, 192], BF16)
        wkv = k.sb("wkv", [128, 4, HG, 256], BF16)
        k.dma("pool", wq[:].rearrange("p a b c -> p (a b c)"), wuq, w=["wq"], sem="wq")
        k.dma("pool", wkv[:].rearrange("p a b c -> p (a b c)"), wukv, w=["wkv"], sem="wkv")
        kro = k.sb("kro", [64, S], BF16)
        k.dma("sp", kro[:], kropeT, w=["kro"], sem="kro")
        freq = k.sb("freq", [32, 1], F32)
        fi = k.sb("fi", [32, 1], F32)
        k.op("pool", lambda e: e.iota(fi[:], [[0, 1]], base=0, channel_multiplier=1, allow_small_or_imprecise_dtypes=True), w=["fi"])
        k.op("act", lambda e: e.activation(out=freq[:], in_=fi[:], func=AF.Exp, scale=-(2.0 / 64.0) * math.log(ROPE_THETA)), r=["fi"], w=["freq"])
        posi = k.sb("posi", [32, NT], I32)
        ang = k.sb("ang", [32, NT], F32)
        angi = k.sb("angi", [32, NT], I32)
        angf = k.sb("angf", [32, NT], F32)
        rt = k.sb("rt", [32, 2, NT], F32)
        cs = [k.sb("cs%d" % i, [32, 2, NT], F32) for i in range(2)]
        for qt in range(NQT):
            sl = slice(qt * NT, (qt + 1) * NT)
            c_ = cs[qt % 2]
            cr = "cs%d" % (qt % 2)
            k.dma("sp", posi[:], pos[:, sl], w=["posi"], sem="posi")
            k.op(V, lambda e: e.tensor_copy(out=ang[:], in_=posi[:]), r=["posi"], w=["ang"])
            for j, sh in enumerate((1.5 * math.pi, math.pi)):
                k.op(V, lambda e: e.tensor_scalar(out=rt[:, j, :], in0=ang[:], scalar1=freq[:, 0:1], scalar2=sh, op0=ALU.mult, op1=ALU.add), r=["ang", "freq"], w=["rt%d" % j])
                emit_range_reduce(k, V, rt[:, j, :], angi[:], angf[:], ["rt%d" % j])
                k.op("act", lambda e: e.activation(out=c_[:, j, :], in_=rt[:, j, :], func=AF.Sin, scale=1.0), r=["rt%d" % j], w=[cr])
            k.op(V, lambda e: e.tensor_scalar(out=c_[:], in0=c_[:], scalar1=ATTN_SCALE, scalar2=None, op0=ALU.mult), r=[cr], w=[cr])
            k.dma("sp", csd[:, :, sl], c_[:], r=[cr], w=["csd"], sem="csd")
        kT = k.sb("kT", [128, S], BF16)
        Vs = k.sb("Vs", [128, NKT, 128], BF16)
        ckt = [k.sb("ckt%d" % i, [128, 4, NT], BF16) for i in range(2)]
        cqt = [k.sb("cqt%d" % i, [128, 4, NT], BF16) for i in range(2)]
        qn = [k.sb("qn%d" % i, [128, NT], BF16) for i in range(2)]
        qr = [k.sb("qr%d" % i, [64, NT], BF16) for i in range(2)]
        t1 = k.sb("t1", [32, NT], F32)
        t2 = k.sb("t2", [32, NT], F32)
        pT = [k.sb("pT%d" % i, [128, NT], BF16) for i in range(3)]
        rinv = k.sb("rinv", [128, NT], F32)
        ot = [k.sb("ot%d" % i, [128, NT], BF16) for i in range(2)]
        pSs = [k.ps("pS%d" % i, [128, NT]) for i in range(2)]
        pO = k.ps("pO", [128, NT])
        pL = k.ps("pL", [128, NT])
        pQn = k.ps("pQn", [128, NT])
        pQ1 = k.ps("pQ1", [128, NT])
        pQ2 = k.ps("pQ2", [128, NT])
        pV = k.ps("pV", [128, 4, 128])
        for h in range(HG):
            for kt in range(S // NT):
                b = kt % 2
                sl = slice(kt * NT, (kt + 1) * NT)
                k.dma("sp", ckt[b][:], ckv[:, :, sl], w=["ckt%d" % b], sem="ckt%d" % b)
                for kc in range(4):
                    k.op("pe", lambda e: e.matmul(pQn[:], lhsT=wkv[:, kc, h, 0:128], rhs=ckt[b][:, kc, :], start=(kc == 0), stop=(kc == 3)), r=["wkv", "ckt%d" % b], w=["pQn"])
                k.op("act", lambda e: e.activation(out=kT[:, sl], in_=pQn[:], func=AF.Copy), r=["pQn"], w=["kT"])
                for j in range(4):
                    for kc in range(4):
                        k.op("pe", lambda e: e.matmul(pV[:, j, :], lhsT=ckt[b][:, kc, j * 128:(j + 1) * 128], rhs=wkv[:, kc, h, 128:256], start=(kc == 0), stop=(kc == 3)), r=["wkv", "ckt%d" % b], w=["pV"])
                k.op(V, lambda e: e.tensor_copy(out=Vs[:, kt * 4:(kt + 1) * 4, :], in_=pV[:]), r=["pV"], w=["Vs"])
            for qt in range(NQT):
                b = qt % 2
                sl = slice(qt * NT, (qt + 1) * NT)
                k.dma("sp", cqt[b][:], cqv[:, :, sl], w=["cqt%d" % b], sem="cqt%d" % b)
                k.dma("sp", cs[b][:], csd[:, :, sl], r=["csd"], w=["cs%d" % b], sem="csl%d" % b)
                for kc in range(4):
                    k.op("pe", lambda e: e.matmul(pQn[:], lhsT=wq[:, kc, h, 0:128], rhs=cqt[b][:, kc, :], start=(kc == 0), stop=(kc == 3)), r=["wq", "cqt%d" % b], w=["pQn"])
                for kc in range(4):
                    k.op("pe", lambda e: e.matmul(pQ1[0:32, :], lhsT=wq[:, kc, h, 128:160], rhs=cqt[b][:, kc, :], start=(kc == 0), stop=(kc == 3)), r=["wq", "cqt%d" % b], w=["pQ1"])
                for kc in range(4):
                    k.op("pe", lambda e: e.matmul(pQ2[0:32, :], lhsT=wq[:, kc, h, 160:192], rhs=cqt[b][:, kc, :], start=(kc == 0), stop=(kc == 3)), r=["wq", "cqt%d" % b], w=["pQ2"])
                k.op("act", lambda e: e.activation(out=qn[b][:], in_=pQn[:], func=AF.Copy, scale=ATTN_SCALE), r=["pQn"], w=["qn%d" % b])
                cosv, sinv = cs[b][:, 0, :], cs[b][:, 1, :]
                crs = "cs%d" % b
                qrr = "qr%d" % b
                k.op(V, lambda e: e.tensor_tensor(out=t1[:], in0=pQ1[0:32, :], in1=cosv, op=ALU.mult), r=["pQ1", crs], w=["t1"])
                k.op(V, lambda e: e.tensor_tensor(out=t2[:], in0=pQ2[0:32, :], in1=sinv, op=ALU.mult), r=["pQ2", crs], w=["t2"])
                k.op(V, lambda e: e.tensor_tensor(out=qr[b][0:32, :], in0=t1[:], in1=t2[:], op=ALU.subtract), r=["t1", "t2"], w=[qrr + "a"])
                k.op(V, lambda e: e.tensor_tensor(out=t1[:], in0=pQ2[0:32, :], in1=cosv, op=ALU.mult), r=["pQ2", crs, qrr + "a"], w=["t1"])
                k.op(V, lambda e: e.tensor_tensor(out=t2[:], in0=pQ1[0:32, :], in1=sinv, op=ALU.mult), r=["pQ1", crs, qrr + "a"], w=["t2"])
                k.op(V, lambda e: e.tensor_tensor(out=qr[b][32:64, :], in0=t1[:], in1=t2[:], op=ALU.add), r=["t1", "t2"], w=[qrr + "b"])
                nkt = 4 * (qt + 1)
                for j in range(nkt):
                    d = j - 4 * qt
                    col0 = 128 * d if d > 0 else 0
                    ks = slice(j * 128, (j + 1) * 128)
                    ps_ = pSs[j % 2][:, col0:NT]
                    psr = "pS%d" % (j % 2)
                    k.op("pe", lambda e: e.matmul(ps_, lhsT=kT[:, ks], rhs=qn[b][:, col0:NT], start=True, stop=False), r=["kT", "qn%d" % b], w=[psr])
                    k.op("pe", lambda e: e.matmul(ps_, lhsT=kro[0:64, ks], rhs=qr[b][0:64, col0:NT], start=False, stop=True), r=["kro", qrr + "a", qrr + "b"], w=[psr])
                    pt_ = pT[j % 3]
                    ptr = "pT%d" % (j % 3)
                    k.op("act", lambda e: e.activation(out=pt_[:, col0:NT], in_=ps_, func=AF.Exp), r=[psr], w=[ptr])
                    if d >= 0:
                        k.op("pool", lambda e: e.memset(pt_[64:128, col0:col0 + 64], 0.0), r=[], w=[ptr])
                    k.op("pe", lambda e: e.matmul(pO[:, col0:NT], lhsT=Vs[:, j, :], rhs=pt_[:, col0:NT], start=(j == 0), stop=(j == nkt - 1)), r=["Vs", ptr], w=["pO"])
                    k.op("pe", lambda e: e.matmul(pL[:, col0:NT], lhsT=ones[:], rhs=pt_[:, col0:NT], start=(j == 0), stop=(j == nkt - 1)), r=["ones", ptr], w=["pL"])
                k.op(V, lambda e: e.reciprocal(out=rinv[:], in_=pL[:]), r=["pL"], w=["rinv"])
                k.op(V, lambda e: e.tensor_tensor(out=ot[b][:], in0=pO[:], in1=rinv[:], op=ALU.mult), r=["pO", "rinv"], w=["ot%d" % b])
                k.dma("pool", attnT[h * 128:(h + 1) * 128, sl], ot[b][:], r=["ot%d" % b], w=["out"], sem="o%d" % b)
        k.finish()
    return nc


def attn_run(cqT, ckvT, kropeT, positions, w_uq, w_ukv):
    B, _, S = cqT.shape
    CPB = NCORE // B
    HG = NH // CPB
    maps = []
    for c in range(NCORE):
        b, q = divmod(c, CPB)
        hs = slice(q * HG, (q + 1) * HG)
        wq = w_uq.reshape(4, 128, NH, 192)[:, :, hs, :].transpose(1, 0, 2, 3).reshape(128, 4 * HG * 192)
        wkv = w_ukv.reshape(4, 128, NH, 256)[:, :, hs, :].transpose(1, 0, 2, 3).reshape(128, 4 * HG * 256)
        maps.append(dict(cqT=np.ascontiguousarray(cqT[b]), ckvT=np.ascontiguousarray(ckvT[b]), kropeT=np.ascontiguousarray(kropeT[b]),
                         pos=np.ascontiguousarray(np.broadcast_to(positions[b][None, :], (32, S))).astype(np.int32),
                         wuq=np.ascontiguousarray(wq), wukv=np.ascontiguousarray(wkv)))
    res = run(get_nc("D", build_D, S, HG), maps)
    out = np.empty((B, 2048, S), NPBF)
    for c in range(NCORE):
        b, q = divmod(c, CPB)
        out[b, q * HG * 128:(q + 1) * HG * 128, :] = res[c]["attnT"]
    return out


def with_halo(aT, b, q, TL):
    F_ = aT.shape[1]
    out = np.zeros((F_, TL + 2), aT.dtype)
    s0 = q * TL
    if q == 0:
        out[:, 2:] = aT[b, :, 0:TL]
    else:
        out[:, :] = aT[b, :, s0 - 2:s0 + TL]
    return out


def ce_run(layer1, xT, mixT, W, vec, positions=None):
    B, _, S = xT.shape
    CPB = NCORE // B
    TL = S // CPB
    wflat, vecs, voff = pack_CE(layer1, W, vec)
    maps = []
    for c in range(NCORE):
        b, q = divmod(c, CPB)
        v = vecs.copy()
        v[:, voff["halo"][0]] = 0.0 if q == 0 else 1.0
        m = dict(xT=with_halo(xT, b, q, TL), mixT=with_halo(mixT, b, q, TL), wflat=wflat, vecs=v)
        if not layer1:
            p = np.zeros((TL + 2,), np.int32)
            if q == 0:
                p[2:] = positions[b, 0:TL]
            else:
                p[:] = positions[b, q * TL - 2:(q + 1) * TL]
            m["pos"] = np.ascontiguousarray(np.broadcast_to(p[None, :], (32, TL + 2)))
        maps.append(m)
    res = run(get_nc("CE", build_CE, TL, layer1), maps)
    names = ["xoT"] + ([] if layer1 else ["ckvT", "kropeT", "cqT"])
    outs = {}
    for n in names:
        F_ = res[0][n].shape[0]
        o = np.empty((B, F_, S), res[0][n].dtype)
        for c in range(NCORE):
            b, q = divmod(c, CPB)
            o[b, :, q * TL:(q + 1) * TL] = res[c][n]
        outs[n] = o
    return outs


def kernel(x, positions, norm_g, ssm_a_re, ssm_a_im, ssm_log_dt, ssm_b_re, ssm_b_im, ssm_c_re, ssm_c_im,
           ssm_d, ssm_w_glu, ssm_b_glu, kv_norm_g, w_dkv, ckv_norm_g, w_ukv, w_dq, cq_norm_g, w_uq, w_o,
           ffn_w_gate, ffn_w_up, ffn_conv_w, ffn_conv_b, ffn_w_down):
    f32 = np.float32
    x = np.asarray(x, f32)
    positions = np.asarray(positions, np.int32)
    A = lambda a: np.asarray(a, f32)
    norm_g = A(norm_g)
    B, S, _ = x.shape
    CPB = NCORE // B
    TL = S // CPB
    xT = np.ascontiguousarray(x.transpose(0, 2, 1))
    maps = []
    for c in range(NCORE):
        b, q = divmod(c, CPB)
        maps.append(dict(xT=np.ascontiguousarray(xT[b, :, q * TL:(q + 1) * TL]), g=pcol(norm_g[0, 0])))
    res = run(get_nc("A", build_A, TL), maps)
    uT = np.empty((B, 2048, S), NPBF)
    for c in range(NCORE):
        b, q = divmod(c, CPB)
        uT[b, :, q * TL:(q + 1) * TL] = res[c]["uT"]
    tabs = s5_tables(A(ssm_a_re)[0], A(ssm_a_im)[0], A(ssm_log_dt)[0], A(ssm_b_re)[0], A(ssm_b_im)[0],
                     A(ssm_c_re)[0], A(ssm_c_im)[0], S // 128)
    ygT = s5_run(uT, tabs, A(ssm_d)[0])
    cw, cb = A(ffn_conv_w), A(ffn_conv_b)
    bglu = A(ssm_b_glu)[0]
    W0 = dict(glu=A(ssm_w_glu)[0], gate=A(ffn_w_gate)[0], up=A(ffn_w_up)[0], down=A(ffn_w_down)[0], dkv=A(w_dkv), dq=A(w_dq)[0])
    v0 = dict(g1=pcol(norm_g[0, 1]), g2=pcol(norm_g[0, 2]), g3=pcol(norm_g[0, 3]), cw0=pcol(cw[0, 0]), cw1=pcol(cw[0, 1]), cw2=pcol(cw[0, 2]),
              cb=pcol(cb[0]), bv=pcol(bglu[:2048]), bg=pcol(bglu[2048:]), kvg=pcol(A(kv_norm_g)), qg=pcol(norm_g[1, 0]),
              ckvg=pcol(A(ckv_norm_g)), cqg=pcol(A(cq_norm_g)[0]))
    oc = ce_run(False, xT, ygT, W0, v0, positions)
    attnT = attn_run(oc["cqT"], oc["ckvT"], oc["kropeT"], positions, A(w_uq)[0], A(w_ukv))
    W1 = dict(wo=A(w_o)[0], gate=A(ffn_w_gate)[1], up=A(ffn_w_up)[1], down=A(ffn_w_down)[1])
    v1 = dict(g1=pcol(norm_g[1, 1]), g2=pcol(norm_g[1, 2]), g3=pcol(norm_g[1, 3]), cw0=pcol(cw[1, 0]), cw1=pcol(cw[1, 1]), cw2=pcol(cw[1, 2]), cb=pcol(cb[1]))
    oe = ce_run(True, oc["xoT"], attnT, W1, v1)
    return np.ascontiguousarray(oe["xoT"].transpose(0, 2, 1)).astype(f32)
```
